# Optimizing a Trainium2 kernel written in Bass

```python
import math
import jax, jax.numpy as jnp
from jax import lax
import numpy as np

D_MODEL = 4096
BATCH = 4
SEQ = 2048
DEPTH = 1

GRID_W = 64
CTX_LEN = 256
HEAD_DIM = 128
N_Q_HEADS = 16
N_KV_HEADS = 4
Q_PER_KV = N_Q_HEADS // N_KV_HEADS
ATTN_WIDTH = N_Q_HEADS * HEAD_DIM
KV_WIDTH = N_KV_HEADS * HEAD_DIM
Q_BLOCK = 128
ROPE_THETA = 10000.0
AXIS_DIM = HEAD_DIM // 2
HYENA_WIDTH = D_MODEL // 2
HYENA_ORDER = 2
SHORT_CONV = 3
FILTER_EMB = 33
FILTER_HIDDEN = 64
FILTER_DIRS = 2
FILTER_INIT_SCALE = 0.1
DECAY_TARGET = 1e-2
FAST_DECAY_PCT = 0.3
SLOW_DECAY_PCT = 1.5
D_FF = 11008
MACARON_W = 0.5
N_MOD = 9
NORM_EPS = 1e-6

Q_END = ATTN_WIDTH
K_END = Q_END + KV_WIDTH
V_END = K_END + KV_WIDTH
HY_END = V_END + (HYENA_ORDER + 1) * HYENA_WIDTH
IN_COLS = HY_END + 2 * D_MODEL

kernel_name = "hybrid_gqa_hyena_macaron_dit_layer"


def rms_norm(x, g):
    x32 = x.astype(jnp.float32)
    y = x32 * lax.rsqrt(jnp.mean(x32 * x32, axis=-1, keepdims=True) + NORM_EPS)
    return (y * g.astype(jnp.float32)).astype(x.dtype)


def modulate(x, g, shift, scale):
    return rms_norm(x, g) * (1.0 + scale) + shift


def grid_positions(n_tokens):
    rows = n_tokens // GRID_W
    row = jnp.repeat(jnp.arange(rows, dtype=jnp.int32), GRID_W)
    col = jnp.tile(jnp.arange(GRID_W, dtype=jnp.int32), rows)
    return row, col


def _rotate(xa, ang):
    x1, x2 = jnp.split(xa, 2, axis=-1)
    cos = jnp.cos(ang)[None, :, None, :]
    sin = jnp.sin(ang)[None, :, None, :]
    return jnp.concatenate([x1 * cos - x2 * sin, x1 * sin + x2 * cos], axis=-1)


def rope_2d(x, row, col):
    inv_freq = ROPE_THETA ** (-jnp.arange(0, AXIS_DIM, 2, dtype=jnp.float32) / AXIS_DIM)
    x32 = x.astype(jnp.float32)
    xr = _rotate(x32[..., :AXIS_DIM], row.astype(jnp.float32)[:, None] * inv_freq)
    xc = _rotate(x32[..., AXIS_DIM:], col.astype(jnp.float32)[:, None] * inv_freq)
    return jnp.concatenate([xr, xc], axis=-1).astype(x.dtype)


def split_heads(p, n_heads):
    return p.reshape(p.shape[0], p.shape[1], n_heads, HEAD_DIM)


def attend(q, k, v):
    s = jnp.einsum('bqkgd,btkd->bkgqt', q, k, preferred_element_type=jnp.float32) / math.sqrt(HEAD_DIM)
    p = jax.nn.softmax(s, axis=-1).astype(v.dtype)
    o = jnp.einsum('bkgqt,btkd->bqkgd', p, v)
    return o.reshape(o.shape[0], o.shape[1], ATTN_WIDTH)


def latent_attention(q, k_all, v_all):
    b, s = q.shape[0], q.shape[1]
    nb = s // Q_BLOCK
    qb = q.reshape(b, nb, Q_BLOCK, N_KV_HEADS, Q_PER_KV, HEAD_DIM).swapaxes(0, 1)
    o = lax.map(lambda blk: attend(blk, k_all, v_all), qb)
    return o.swapaxes(0, 1).reshape(b, s, ATTN_WIDTH)


def kv_heads(p_kv, k_norm_g):
    p_k, p_v = jnp.split(p_kv, 2, axis=-1)
    return rms_norm(split_heads(p_k, N_KV_HEADS), k_norm_g), split_heads(p_v, N_KV_HEADS)


def hyena_filters(n, w1, b1, w2, b2, w3, b3, freq, w_out):
    t = jnp.linspace(0.0, 1.0, n, dtype=jnp.float32)[:, None]
    bands = (FILTER_EMB - 1) // 2
    w = 2.0 * math.pi * jnp.arange(n, dtype=jnp.float32)[:, None] / n
    f = jnp.linspace(1e-4, bands - 1, bands, dtype=jnp.float32)[None, :]
    z = jnp.concatenate([t, jnp.cos(f * w), -jnp.sin(f * w)], axis=-1)
    a = jnp.sin(freq * (z @ w1 + b1))
    a = jnp.sin(freq * (a @ w2 + b2))
    a = jnp.sin(freq * (a @ w3 + b3))
    h = (a @ w_out).astype(jnp.float32).reshape(n, HYENA_ORDER, FILTER_DIRS, HYENA_WIDTH)
    max_decay = math.log(DECAY_TARGET) / FAST_DECAY_PCT
    min_decay = math.log(DECAY_TARGET) / SLOW_DECAY_PCT
    deltas = jnp.linspace(min_decay, max_decay, HYENA_WIDTH, dtype=jnp.float32)
    decay = jnp.exp(-t * jnp.abs(deltas))
    return h * decay[:, None, None, :]


def long_conv(u, h_fwd, h_bwd, bias):
    n, ch = u.shape[1], u.shape[2]
    k = jnp.concatenate([h_fwd, jnp.zeros((1, ch), jnp.float32), h_bwd[:0:-1]], axis=0)
    u32 = u.astype(jnp.float32)
    y = jnp.fft.irfft(jnp.fft.rfft(u32, n=2 * n, axis=1) * jnp.fft.rfft(k, axis=0)[None],
                      n=2 * n, axis=1)[:, :n]
    return (y + u32 * bias.astype(jnp.float32)).astype(u.dtype)


def short_conv(u, w, b):
    y = lax.conv_general_dilated(u, w[:, None, :], window_strides=(1,),
                                 padding=[(SHORT_CONV // 2, SHORT_CONV // 2)],
                                 dimension_numbers=('NWC', 'WIO', 'NWC'),
                                 feature_group_count=u.shape[-1])
    return y + b


def hyena_mixer(p_hy, w_short, b_short, filters, hy_bias):
    u = short_conv(p_hy, w_short, b_short)
    v, x1, x2 = jnp.split(u, HYENA_ORDER + 1, axis=-1)
    z = x1 * long_conv(v, filters[:, 0, 0], filters[:, 0, 1], hy_bias[0])
    return x2 * long_conv(z, filters[:, 1, 0], filters[:, 1, 1], hy_bias[1])


def merge_branches(attn_o, hy_o, p_gates, w_br_attn, w_br_hyena, w_out):
    g_a, g_h = jnp.split(p_gates, 2, axis=-1)
    merged = jax.nn.sigmoid(g_a) * (attn_o @ w_br_attn) + jax.nn.sigmoid(g_h) * (hy_o @ w_br_hyena)
    return merged @ w_out


def swiglu_sublayer(x, shift, scale, gate, pre_g, post_g, w_gate, w_up, w_down):
    h = modulate(x, pre_g, shift, scale)
    f = (jax.nn.silu(h @ w_gate) * (h @ w_up)) @ w_down
    return x + MACARON_W * gate * rms_norm(f, post_g)


def setup_inputs(seed: int = 0) -> dict:
    key = jax.random.key(seed)
    ks = jax.random.split(key, 32)

    def nrm(k, shape, scale):
        return jax.random.normal(k, shape, jnp.float32) * scale

    D, L, HW = D_MODEL, DEPTH, HYENA_WIDTH
    return {
        "x": nrm(ks[0], (BATCH, SEQ, D), 1.0),
        "c": nrm(ks[1], (BATCH, D), 1.0),
        "ctx": nrm(ks[2], (BATCH, CTX_LEN, D), 1.0),
        "c_ctx": nrm(ks[3], (D,), 1.0),
        "w_ada": nrm(ks[4], (L, D, N_MOD * D), D ** -0.5),
        "b_ada": nrm(ks[5], (L, N_MOD * D), 0.01),
        "pre_g": 1.0 + nrm(ks[6], (L, 3, D), 0.02),
        "post_g": 1.0 + nrm(ks[7], (L, 3, D), 0.02),
        "ffn_w_gate": nrm(ks[8], (L, 2, D, D_FF), D ** -0.5),
        "ffn_w_up": nrm(ks[9], (L, 2, D, D_FF), D ** -0.5),
        "ffn_w_down": nrm(ks[10], (L, 2, D_FF, D), D_FF ** -0.5),
        "w_in": nrm(ks[11], (L, D, IN_COLS), D ** -0.5),
        "q_norm_g": 1.0 + nrm(ks[12], (L, HEAD_DIM), 0.02),
        "k_norm_g": 1.0 + nrm(ks[13], (L, HEAD_DIM), 0.02),
        "short_w": nrm(ks[14], (L, SHORT_CONV, (HYENA_ORDER + 1) * HW), SHORT_CONV ** -0.5),
        "short_b": nrm(ks[15], (L, (HYENA_ORDER + 1) * HW), 0.01),
        "filt_w1": nrm(ks[16], (L, FILTER_EMB, FILTER_HIDDEN), FILTER_EMB ** -0.5),
        "filt_b1": nrm(ks[17], (L, FILTER_HIDDEN), 0.01),
        "filt_w2": nrm(ks[18], (L, FILTER_HIDDEN, FILTER_HIDDEN), FILTER_HIDDEN ** -0.5),
        "filt_b2": nrm(ks[19], (L, FILTER_HIDDEN), 0.01),
        "filt_w3": nrm(ks[20], (L, FILTER_HIDDEN, FILTER_HIDDEN), FILTER_HIDDEN ** -0.5),
        "filt_b3": nrm(ks[21], (L, FILTER_HIDDEN), 0.01),
        "filt_freq": 1.0 + nrm(ks[22], (L, FILTER_HIDDEN), 0.02),
        "filt_w_out": nrm(ks[23], (L, FILTER_HIDDEN, HYENA_ORDER * FILTER_DIRS * HW),
                           FILTER_INIT_SCALE * FILTER_HIDDEN ** -0.5),
        "hyena_bias": nrm(ks[24], (L, HYENA_ORDER, HW), 0.1),
        "w_br_attn": nrm(ks[25], (L, ATTN_WIDTH, D), ATTN_WIDTH ** -0.5),
        "w_br_hyena": nrm(ks[26], (L, HW, D), HW ** -0.5),
        "w_out": nrm(ks[27], (L, D, D), D ** -0.5),
    }


def reference(x, c, ctx, c_ctx, w_ada, b_ada, pre_g, post_g, ffn_w_gate, ffn_w_up, ffn_w_down,
              w_in, q_norm_g, k_norm_g, short_w, short_b, filt_w1, filt_b1, filt_w2, filt_b2,
              filt_w3, filt_b3, filt_freq, filt_w_out, hyena_bias, w_br_attn, w_br_hyena, w_out):
    n_lat = x.shape[1]
    n_ctx = ctx.shape[1]
    row, col = grid_positions(n_lat)
    for l in range(DEPTH):
        last = l == DEPTH - 1
        m = (jax.nn.silu(c) @ w_ada[l] + b_ada[l]).reshape(c.shape[0], N_MOD, 1, D_MODEL)
        mc = (jax.nn.silu(c_ctx) @ w_ada[l] + b_ada[l]).reshape(N_MOD, D_MODEL)

        x = swiglu_sublayer(x, m[:, 0], m[:, 1], m[:, 2], pre_g[l, 0], post_g[l, 0],
                            ffn_w_gate[l, 0], ffn_w_up[l, 0], ffn_w_down[l, 0])
        ctx = swiglu_sublayer(ctx, mc[0], mc[1], mc[2], pre_g[l, 0], post_g[l, 0],
                              ffn_w_gate[l, 0], ffn_w_up[l, 0], ffn_w_down[l, 0])

        h = modulate(x, pre_g[l, 1], m[:, 3], m[:, 4])
        hc = modulate(ctx, pre_g[l, 1], mc[3], mc[4])
        p = h @ w_in[l]
        if last:
            pc_kv = hc @ w_in[l][:, Q_END:V_END]
        else:
            pc = hc @ w_in[l]
            pc_kv = pc[..., Q_END:V_END]
        kc, vc = kv_heads(pc_kv, k_norm_g[l])

        q = rope_2d(rms_norm(split_heads(p[..., :Q_END], N_Q_HEADS), q_norm_g[l]), row, col)
        k, v = kv_heads(p[..., Q_END:V_END], k_norm_g[l])
        k = rope_2d(k, row, col)
        k_all = jnp.concatenate([kc, k], axis=1)
        v_all = jnp.concatenate([vc, v], axis=1)
        attn_o = latent_attention(q, k_all, v_all)

        filt = (filt_w1[l], filt_b1[l], filt_w2[l], filt_b2[l], filt_w3[l], filt_b3[l],
                filt_freq[l], filt_w_out[l])
        hy_o = hyena_mixer(p[..., V_END:HY_END], short_w[l], short_b[l],
                           hyena_filters(n_lat, *filt), hyena_bias[l])

        out = merge_branches(attn_o, hy_o, p[..., HY_END:], w_br_attn[l], w_br_hyena[l], w_out[l])
        x = x + m[:, 5] * rms_norm(out, post_g[l, 1])

        if not last:
            qc = rms_norm(split_heads(pc[..., :Q_END], N_Q_HEADS), q_norm_g[l])
            attn_c = attend(qc.reshape(qc.shape[0], n_ctx, N_KV_HEADS, Q_PER_KV, HEAD_DIM), kc, vc)
            hy_c = hyena_mixer(pc[..., V_END:HY_END], short_w[l], short_b[l],
                               hyena_filters(n_ctx, *filt), hyena_bias[l])
            out_c = merge_branches(attn_c, hy_c, pc[..., HY_END:], w_br_attn[l], w_br_hyena[l], w_out[l])
            ctx = ctx + mc[5] * rms_norm(out_c, post_g[l, 1])
            ctx = swiglu_sublayer(ctx, mc[6], mc[7], mc[8], pre_g[l, 2], post_g[l, 2],
                                  ffn_w_gate[l, 1], ffn_w_up[l, 1], ffn_w_down[l, 1])

        x = swiglu_sublayer(x, m[:, 6], m[:, 7], m[:, 8], pre_g[l, 2], post_g[l, 2],
                            ffn_w_gate[l, 1], ffn_w_up[l, 1], ffn_w_down[l, 1])
    return x
```

```python
import math
import contextlib
import numpy as np
import ml_dtypes
import concourse.bass as bass
import concourse.mybir as mybir
from concourse.bass_utils import run_bass_kernel_spmd

F32 = mybir.dt.float32
BF16 = mybir.dt.bfloat16
AF = mybir.ActivationFunctionType
ALU = mybir.AluOpType
AX = mybir.AxisListType


ENGS = ("pe", "act", "dve", "pool", "sp")
NSLOT = 8


class _Op:
    __slots__ = ("eng", "fn", "deps", "sig", "dma", "slot", "val", "gidx", "waits")


class Sched:
    def __init__(self, nc, stack):
        self.nc = nc
        self.sem = {e: stack.enter_context(nc.semaphore("s_" + e)) for e in ENGS}
        self.dsem = {e: [stack.enter_context(nc.semaphore("d_%s%d" % (e, i))) for i in range(NSLOT)]
                     for e in ("sp", "pool", "act")}
        self.cnt = {e: 0 for e in ENGS}
        self.dcnt = {e: 0 for e in self.dsem}
        self.waited = {e: {} for e in ENGS}
        self.last_w = {}
        self.readers = {}
        self.ops = []
        self.gidx = 0
        self.all_dma = []
        self.recent = {e: [] for e in self.dsem}
        self.last_op = {e: None for e in ENGS}

    def op(self, eng, fn, reads=(), writes=(), dma=False, extra_deps=()):
        o = _Op()
        o.eng, o.fn, o.dma, o.sig = eng, fn, dma, dma
        o.gidx = self.gidx
        self.gidx += 1

        def _flat(keys):
            out = []
            for k in keys:
                if isinstance(k, list):
                    out.extend(k)
                else:
                    out.append(k)
            return out
        reads, writes = _flat(reads), _flat(writes)
        deps = set(extra_deps)
        for k in reads:
            w = self.last_w.get(k)
            if w is not None:
                deps.add(w)
        for k in writes:
            w = self.last_w.get(k)
            if w is not None:
                deps.add(w)
            for r in self.readers.get(k, ()):
                deps.add(r)
        for k in reads:
            self.readers.setdefault(k, []).append(o)
        for k in writes:
            self.last_w[k] = o
            self.readers[k] = []
        deps.discard(o)
        o.deps = [d for d in deps if not (d.eng == "pe" and eng == "pe" and not d.dma and not dma)]
        for d in o.deps:
            d.sig = True
        if dma:
            q = eng
            i = self.dcnt[q]
            self.dcnt[q] += 1
            o.slot, o.val = i % NSLOT, 16 * (i // NSLOT + 1)
            self.all_dma.append(o)
            self.recent[q] = (self.recent[q] + [o])[-NSLOT:]
        self.ops.append(o)
        self.last_op[eng] = o
        return o

    def barrier(self):
        lasts = [o for o in self.last_op.values() if o is not None]
        for q in self.recent:
            lasts += self.recent[q]
        outs = []
        for e in ENGS:
            o = self.op(e, None, extra_deps=lasts)
            o.sig = True
            outs.append(o)
        self.last_w = {}
        self.readers = {}
        return outs

    def flush(self, name=None):
        nc = self.nc
        ops = self.ops
        self.ops = []
        for o in ops:
            if o.sig and not o.dma:
                self.cnt[o.eng] += 1
                o.val = self.cnt[o.eng]
        per = {e: [o for o in ops if o.eng == e] for e in ENGS}

        def emit(e, engine):
            wt = self.waited[e]
            for o in per[e]:
                need = {}
                for d in o.deps:
                    if d.dma:
                        key = ("d", d.eng, d.slot)
                    else:
                        key = ("c", d.eng)
                    if d.val > need.get(key, 0):
                        need[key] = d.val
                if o.dma and o.val > 16:
                    key = ("d", e, o.slot)
                    need[key] = max(need.get(key, 0), o.val - 16)
                for key, v in need.items():
                    if wt.get(key, 0) >= v:
                        continue
                    wt[key] = v
                    s = self.dsem[key[1]][key[2]] if key[0] == "d" else self.sem[key[1]]
                    engine.wait_ge(s, v)
                if o.fn is None:
                    if o.sig:
                        engine.nop().then_inc(self.sem[e], 1)
                    continue
                ins = o.fn(engine)
                if o.dma:
                    ins.then_inc(self.dsem[e][o.slot], 16)
                elif o.sig:
                    ins.then_inc(self.sem[e], 1)

        with nc.Block(name) as block:
            @block.tensor
            def _(eng):
                emit("pe", eng)

            @block.scalar
            def _(eng):
                emit("act", eng)

            @block.vector
            def _(eng):
                emit("dve", eng)

            @block.gpsimd
            def _(eng):
                emit("pool", eng)

            @block.sync
            def _(eng):
                emit("sp", eng)


class Cfg:
    def __init__(self, D=4096, DFF=11008, NQ=16, NKV=4, SEQ=2048, CTX=256, GRID_W=64, B=4):
        self.D, self.DFF, self.NQ, self.NKV, self.SEQ, self.CTX, self.GRID_W, self.B = D, DFF, NQ, NKV, SEQ, CTX, GRID_W, B
        self.KT = D // 128
        self.FC = DFF // 128
        self.AW = NQ * 128
        self.KVW = NKV * 128
        self.HW = D // 2
        self.HC = self.HW // 128
        self.NLOC = SEQ // 2
        self.LS = self.NLOC // 128
        self.CS = CTX // 128
        self.SS = SEQ // 128
        self.FS = self.CS + self.SS
        self.Q_END = self.AW
        self.K_END = self.Q_END + self.KVW
        self.V_END = self.K_END + self.KVW
        self.HY_END = self.V_END + 3 * self.HW
        self.IN_COLS = self.HY_END + 2 * D
        self.CB = min(512, self.HW)
        self.NCB = self.HW // self.CB


NB = 8


class MK:
    def __init__(self, cfg):
        self.c = cfg
        self.nc = bass.Bass("TRN2", target_bir_lowering=False)
        self.dram = {}

    def din(self, name, shape, dt=F32):
        self.dram[name] = self.nc.dram_tensor(name, list(shape), dt, kind="ExternalInput").ap()
        return self.dram[name]

    def dscr(self, name, shape, dt=F32):
        self.dram[name] = self.nc.dram_tensor(name, list(shape), dt, kind="Internal").ap()
        return self.dram[name]

    def build(self, debug_outs=()):
        c, nc = self.c, self.nc
        D, KT, FS = c.D, c.KT, c.FS
        d = self.dram
        self.din("xfull", [FS * 128, D]); self.din("cvec", [2, D])
        self.din("w_ada", [D, 9 * D]); self.din("b_ada", [9 * D]); self.din("pre_g", [3, D]); self.din("post_g", [3, D])
        self.din("wg", [2, D, c.DFF]); self.din("wu", [2, D, c.DFF]); self.din("wd", [2, c.DFF, D])
        self.din("w_in", [D, c.IN_COLS]); self.din("qg4", [512]); self.din("kg4", [512])
        self.din("short_w", [3, 3 * c.HW]); self.din("short_b", [3 * c.HW])
        self.din("fw1", [33, 64]); self.din("fb1", [64]); self.din("fw2", [64, 64]); self.din("fb2", [64])
        self.din("fw3", [64, 64]); self.din("fb3", [64]); self.din("ffreq", [64]); self.din("fwo", [64, 4 * c.HW])
        self.din("hbias", [2, c.HW]); self.din("w_bra", [c.AW, D]); self.din("w_brh", [c.HW, D]); self.din("w_out", [D, D])
        self.din("ident", [128, 128]); self.din("onesb", [128, 128], BF16)
        self.din("ropec", [FS * 128, 512]); self.din("ropes", [FS * 128, 512])
        self.din("tabf", [c.SS, 128, 2 * c.SS * 128], BF16); self.din("tabg", [c.SS, 128, 2 * c.SS * 128], BF16)
        self.din("zt", [33, c.SEQ]); self.din("lagc", [128, 3 * c.SS]); self.din("adelta", [c.HW])
        self.out = nc.dram_tensor("out", [c.NLOC, D], F32, kind="ExternalOutput").ap()
        self.dscr("HT", [128, KT, FS * 128], BF16); self.dscr("F", [FS * 128, D]); self.dscr("X1", [FS * 128, D])
        self.dscr("X2", [c.NLOC, D]); self.dscr("PHY", [c.SEQ + 2, 3 * c.HW]); self.dscr("GATES", [c.NLOC, 2 * D])
        self.dscr("KSP", [2, c.NCB, c.SS, 128, 3 * c.CB])
        self.dscr("QT", [128, c.NQ * c.NLOC], BF16); self.dscr("KTS", [128, c.NKV * FS * 128], BF16)
        self.dscr("VT", [FS * 128, c.KVW], BF16); self.dscr("AOT", [128, c.NQ * c.NLOC], BF16)
        self.dscr("HYT", [128, c.HC * c.NLOC], BF16)
        self.dbg = {}
        for nm, shape in debug_outs:
            self.dbg[nm] = nc.dram_tensor("dbg_" + nm, list(shape), F32, kind="ExternalOutput").ap()

        with contextlib.ExitStack() as st:
            self.S = S = Sched(nc, st)
            self._uid = 0

            def sb(name, shape, dt=F32, stack=st):
                self._uid += 1
                return stack.enter_context(nc.sbuf_tensor("sb%d_%s" % (self._uid, name), list(shape), dt))
            self.sb = sb
            self.pb = [st.enter_context(nc.psum_tensor("pb%d" % i, [128, 512], F32)) for i in range(8)]
            self.wp = [sb("wp%d" % i, [128, 4096], BF16) for i in range(NB)]
            self.wi = 0
            self.ident = sb("ident", [128, 128])
            self.onesb = sb("onesb", [128, 128], BF16)
            self.Mfm = sb("Mfm", [128, 9 * KT, 2])
            self.Gm = sb("Gm", [128, 3, KT, 2]); self.PGm = sb("PGm", [128, 3, KT, 2])
            self.ssq = sb("ssq", [128, FS, 8]); self.rstd = sb("rstd", [128, FS])
            S.op("sp", lambda e: e.dma_start(out=self.ident[:], in_=d["ident"]), writes=["ident"], dma=True)
            S.op("pool", lambda e: e.dma_start(out=self.onesb[:], in_=d["onesb"]), writes=["onesb"], dma=True)
            self.end_stage()
            import os
            allsub = [(i, 1 if i < c.CS else 0) for i in range(FS)]
            locsub = [(i, 0) for i in range(c.LS)]
            st2 = contextlib.ExitStack()
            seq = [
                ("adaln", lambda: self.stage_adaln()),
                ("filters", lambda: self.stage_filters()),
                ("pre0", lambda: self.stage_pre(d["xfull"], allsub, 0)),
                ("ffn0", lambda: self.stage_ffn(0, FS)),
                ("post0", lambda: (self.stage_post(d["xfull"], d["X1"], allsub, 0), self.dbg_copy("X1", d["X1"]))),
                ("pre1", lambda: self.stage_pre(d["X1"], allsub, 1)),
                ("proj", lambda: self.stage_proj()),
                ("attn", lambda: self.stage_attn()),
                ("hyena", lambda: self.stage_hyena()),
                ("merge", lambda: self.stage_merge()),
                ("post1", lambda: self.stage_post(d["X1"][c.CTX:c.CTX + c.NLOC, :], d["X2"], locsub, 1)),
                ("pre2", lambda: self.stage_pre(d["X2"], locsub, 2)),
                ("ffn1", lambda: self.stage_ffn(1, c.LS)),
                ("post2", lambda: self.stage_post(d["X2"], self.out, locsub, 2, final=True)),
            ]
            stop = os.environ.get("MK_STOP", "")
            for nm, fn in seq:
                fn()
                if nm == stop:
                    break
            st2.close()
        return nc

    def dbg_copy(self, name, src):
        if name in self.dbg:
            self.S.op("sp", lambda e: e.dma_start(out=self.dbg[name], in_=src), dma=True)
            self.end_stage()

    def stage_proj(self):
        c, S, d = self.c, self.S, self.dram
        D, KT, FS, CS, LS = c.D, c.KT, c.FS, c.CS, c.LS
        w_in = d["w_in"]
        blocks = []
        for i in range(c.AW // 512):
            blocks.append(("q", i * 512, 512, i))
        blocks.append(("k", c.Q_END, c.KVW, 0))
        blocks.append(("v", c.K_END, c.KVW, 0))
        hw = [w_ for w_ in (512, 384, 256, 128) if (3 * c.HW) % w_ == 0][0]
        for i in range(3 * c.HW // hw):
            blocks.append(("h", c.V_END + i * hw, hw, i))
        for i in range(2 * D // 512):
            blocks.append(("g", c.HY_END + i * 512, 512, i))
        tiles = []
        def mk_tiles(lo, hi, kinds):
            for t0 in range(lo, hi, 8):
                tiles.append((list(range(t0, min(hi, t0 + 8))), kinds))
        mk_tiles(0, CS, "kv")
        mk_tiles(CS, CS + LS, "qkvhg")
        mk_tiles(CS + LS, FS, "kvh")
        with contextlib.ExitStack() as st:
            sb = lambda name, shape, dt=F32: self.sb(name, shape, dt, st)
            TT = 1024
            hT = sb("hT", [128, KT * TT], BF16)
            hT3 = hT[:, :].rearrange("p (k t) -> p k t", k=KT)
            gq = sb("gq", [128, 512]); gk = sb("gk", [128, 512])
            zrow = sb("zrow", [128, 3 * c.HW // 128])
            cs_t = [sb("cs%d" % i, [128, 512]) for i in range(2)]; sn_t = [sb("sn%d" % i, [128, 512]) for i in range(2)]
            xg = [sb("xg%d" % i, [128, 512]) for i in range(2)]; xs = [sb("xs%d" % i, [128, 512]) for i in range(2)]
            stq = sb("stq", [128, 8]); junk = sb("junk", [128, 128], BF16)
            qst = [sb("qst%d" % i, [128, 512], BF16) for i in range(2)]
            ev = [sb("ev%d" % i, [128, 512]) for i in range(3)]
            evb = [sb("evb%d" % i, [128, 512], BF16) for i in range(2)]
            S.op("sp", lambda e: e.dma_start(out=gq[:], in_=d["qg4"].partition_broadcast(128)), writes=["gq"], dma=True)
            S.op("sp", lambda e: e.dma_start(out=gk[:], in_=d["kg4"].partition_broadcast(128)), writes=["gk"], dma=True)
            S.op("dve", lambda e: e.memset(zrow[:], 0.0), writes=["zrow"])
            for r in (0, c.SEQ + 1):
                S.op("sp", lambda e, r=r: e.dma_start(out=d["PHY"][r, :].rearrange("(p j) -> p j", p=128), in_=zrow[:]), reads=["zrow"], writes=[("PHYz", r)], dma=True)
            cnt = {"bank": 0, "n": 0, "ev": 0, "evb": 0}
            KG = min(KT, 4096 // 512)
            NQB = KT // KG
            for subs, kinds in tiles:
                ts = len(subs); T = ts * 128; t0 = subs[0]
                S.op("sp", lambda e, t0=t0, T=T: e.dma_start(out=hT3[:, :, 0:T], in_=d["HT"][:, :, t0 * 128:t0 * 128 + T]),
                     reads=[("HT", t0 + i) for i in range(ts)], writes=["hT"], dma=True)
                for kind, c0, W, bi in blocks:
                    if kind not in kinds:
                        continue
                    bufs = []
                    for q in range(NQB):
                        src = w_in[q * KG * 128:(q + 1) * KG * 128, c0:c0 + W].rearrange("(kt p) j -> p kt j", p=128)
                        bufs.append(self.wload(src, "p (kt j) -> p kt j", n=KG * W, kt=KG))
                    for s_ in range(ts):
                        si = subs[s_]
                        bank = cnt["bank"] % 6; cnt["bank"] += 1

                        def f(e, bufs=bufs, s_=s_, bank=bank, W=W):
                            ins = None
                            for q in range(NQB):
                                for kt in range(KG):
                                    ins = e.matmul(self.pb[bank][:, 0:W], lhsT=hT[:, (q * KG + kt) * TT + s_ * 128:(q * KG + kt) * TT + (s_ + 1) * 128],
                                                   rhs=bufs[q][0][:, kt, :], start=(q == 0 and kt == 0), stop=(q == NQB - 1 and kt == KG - 1))
                            return ins
                        S.op("pe", f, reads=[b[1] for b in bufs] + ["hT"], writes=[("pb", bank)])
                        pk = ("pb", bank)
                        if kind in ("q", "k"):
                            n = cnt["n"] % 2; cnt["n"] += 1
                            nh = W // 128
                            gt = gq if kind == "q" else gk
                            S.op("sp", lambda e, n=n, si=si, W=W: e.dma_start(out=cs_t[n][:, 0:W], in_=d["ropec"][si * 128:(si + 1) * 128, 0:W]), writes=[("cs", n)], dma=True)
                            S.op("sp", lambda e, n=n, si=si, W=W: e.dma_start(out=sn_t[n][:, 0:W], in_=d["ropes"][si * 128:(si + 1) * 128, 0:W]), writes=[("sn", n)], dma=True)
                            S.op("dve", lambda e: e.memset(stq[:], 0.0), writes=["stq"])
                            for h in range(nh):
                                S.op("act", lambda e, h=h, bank=bank: e.activation(out=junk[:], in_=self.pb[bank][:, h * 128:(h + 1) * 128], func=AF.Square, accum_out=stq[:, h:h + 1]),
                                     reads=[pk, "stq"], writes=["junk", "stq"])
                            S.op("dve", lambda e, nh=nh: e.tensor_scalar(stq[:, 4:4 + nh], stq[:, 0:nh], 1.0 / 128, 1e-6, ALU.mult, ALU.add), reads=["stq"], writes=["stq"])
                            S.op("act", lambda e, nh=nh: e.activation(out=stq[:, 4:4 + nh], in_=stq[:, 4:4 + nh], func=AF.Sqrt), reads=["stq"], writes=["stq"])
                            S.op("dve", lambda e, nh=nh: e.reciprocal(stq[:, 4:4 + nh], stq[:, 4:4 + nh]), reads=["stq"], writes=["stq"])
                            for h in range(nh):
                                S.op("act", lambda e, h=h, bank=bank, n=n: e.activation(out=xg[n][:, h * 128:(h + 1) * 128], in_=self.pb[bank][:, h * 128:(h + 1) * 128],
                                                                                     func=AF.Copy, scale=stq[:, 4 + h:5 + h]), reads=[pk, "stq"], writes=[("xg", n)])
                            S.op("dve", lambda e, n=n, W=W, gt=gt: e.tensor_tensor(xg[n][:, 0:W], xg[n][:, 0:W], gt[:, 0:W], ALU.mult), reads=[("xg", n), "gq", "gk"], writes=[("xg", n)])
                            xv = xg[n][:, 0:W].rearrange("p (a two k) -> p a two k", two=2, k=32)
                            sv = xs[n][:, 0:W].rearrange("p (a two k) -> p a two k", two=2, k=32)
                            S.op("act", lambda e, xv=xv, sv=sv: e.activation(out=sv[:, :, 0, :], in_=xv[:, :, 1, :], func=AF.Copy), reads=[("xg", n)], writes=[("xs", n)])
                            S.op("act", lambda e, xv=xv, sv=sv: e.activation(out=sv[:, :, 1, :], in_=xv[:, :, 0, :], func=AF.Copy), reads=[("xg", n)], writes=[("xs2", n)])
                            S.op("dve", lambda e, n=n, W=W: e.tensor_tensor(xg[n][:, 0:W], xg[n][:, 0:W], cs_t[n][:, 0:W], ALU.mult), reads=[("xg", n), ("xs", n), ("xs2", n), ("cs", n)], writes=[("xg", n)])
                            S.op("dve", lambda e, n=n, W=W: e.tensor_tensor(xs[n][:, 0:W], xs[n][:, 0:W], sn_t[n][:, 0:W], ALU.mult), reads=[("xs", n), ("xs2", n), ("sn", n)], writes=[("xs", n), ("xs2", n)])
                            S.op("dve", lambda e, n=n, W=W: e.tensor_tensor(xg[n][:, 0:W], xg[n][:, 0:W], xs[n][:, 0:W], ALU.add), reads=[("xg", n), ("xs", n)], writes=[("xg", n)])
                            tb = 6 + n

                            def ft(e, n=n, nh=nh, tb=tb):
                                ins = None
                                for h in range(nh):
                                    ins = e.transpose(self.pb[tb][:, h * 128:(h + 1) * 128], xg[n][:, h * 128:(h + 1) * 128], self.ident[:])
                                return ins
                            S.op("pe", ft, reads=[("xg", n), "ident"], writes=[("pb", tb)])
                            S.op("act", lambda e, n=n, W=W, tb=tb: e.activation(out=qst[n][:, 0:W], in_=self.pb[tb][:, 0:W], func=AF.Copy), reads=[("pb", tb)], writes=[("qst", n)])
                            if kind == "q":
                                tl = (si - CS) * 128
                                dst = d["QT"].rearrange("p (h t) -> p h t", h=c.NQ)[:, bi * 4:bi * 4 + nh, tl:tl + 128]
                            else:
                                dst = d["KTS"].rearrange("p (h t) -> p h t", h=c.NKV)[:, 0:nh, si * 128:(si + 1) * 128]
                            S.op("sp", lambda e, n=n, W=W, dst=dst, nh=nh: e.dma_start(out=dst, in_=qst[n][:, 0:W].rearrange("p (h t) -> p h t", h=nh)),
                                 reads=[("qst", n)], writes=[("QK", kind, si, bi)], dma=True)
                        elif kind == "v":
                            n = cnt["evb"] % 2; cnt["evb"] += 1
                            S.op("act", lambda e, n=n, W=W, bank=bank: e.activation(out=evb[n][:, 0:W], in_=self.pb[bank][:, 0:W], func=AF.Copy), reads=[pk], writes=[("evb", n)])
                            S.op("sp", lambda e, n=n, W=W, si=si: e.dma_start(out=d["VT"][si * 128:(si + 1) * 128, 0:W], in_=evb[n][:, 0:W]), reads=[("evb", n)], writes=[("VT", si)], dma=True)
                        else:
                            n = cnt["ev"] % 3; cnt["ev"] += 1
                            fn_ = AF.Copy if kind == "h" else AF.Sigmoid
                            S.op("act", lambda e, n=n, W=W, bank=bank, fn_=fn_: e.activation(out=ev[n][:, 0:W], in_=self.pb[bank][:, 0:W], func=fn_), reads=[pk], writes=[("ev", n)])
                            if kind == "h":
                                r0 = 1 + (si - CS) * 128
                                dst = d["PHY"][r0:r0 + 128, c0 - c.V_END:c0 - c.V_END + W]
                            else:
                                r0 = (si - CS) * 128
                                dst = d["GATES"][r0:r0 + 128, c0 - c.HY_END:c0 - c.HY_END + W]
                            S.op("sp", lambda e, n=n, W=W, dst=dst: e.dma_start(out=dst, in_=ev[n][:, 0:W]), reads=[("ev", n)], writes=[("PG", kind, si, bi)], dma=True)
            self.end_stage()

    def stage_attn(self):
        c, S, d = self.c, self.S, self.dram
        FS, NLOC = c.FS, c.NLOC
        QB = min(512, NLOC); NQBK = NLOC // QB
        G = c.NQ // c.NKV
        with contextlib.ExitStack() as st:
            sb = lambda name, shape, dt=F32: self.sb(name, shape, dt, st)
            kT = sb("kT", [128, c.NKV * FS * 128], BF16)
            vv = sb("vv", [128, FS * c.KVW], BF16)
            qt = [sb("qt%d" % i, [128, QB], BF16) for i in range(2)]
            pT = [sb("pT%d" % i, [128, QB], BF16) for i in range(3)]
            o32 = [sb("o32_%d" % i, [128, QB]) for i in range(2)]; d32 = [sb("d32_%d" % i, [128, QB]) for i in range(2)]
            ob = [sb("ob%d" % i, [128, QB], BF16) for i in range(2)]
            S.op("sp", lambda e: e.dma_start(out=kT[:], in_=d["KTS"]), writes=["kT"], dma=True)
            S.op("sp", lambda e: e.dma_start(out=vv[:, :].rearrange("p (kb w) -> p kb w", kb=FS), in_=d["VT"].rearrange("(kb p) w -> p kb w", p=128)), writes=["vv"], dma=True)
            it = 0
            pi = 0
            for h in range(c.NQ):
                kv = h // G
                for qb in range(NQBK):
                    n = it % 2; it += 1
                    ob_, db_ = 2 + 2 * n, 3 + 2 * n
                    S.op("sp", lambda e, n=n, h=h, qb=qb: e.dma_start(out=qt[n][:], in_=d["QT"][:, h * NLOC + qb * QB:h * NLOC + (qb + 1) * QB]), writes=[("qt", n)], dma=True)

                    def sc(kb, n=n, kv=kv):
                        bank = kb % 2
                        S.op("pe", lambda e, kb=kb, bank=bank: e.matmul(self.pb[bank][:, 0:QB], lhsT=kT[:, kv * FS * 128 + kb * 128:kv * FS * 128 + (kb + 1) * 128],
                                                                      rhs=qt[n][:], start=True, stop=True), reads=["kT", ("qt", n)], writes=[("pb", bank)])
                    sc(0)
                    for kb in range(FS):
                        if kb + 1 < FS:
                            sc(kb + 1)
                        r = pi % 3; pi += 1
                        bank = kb % 2
                        S.op("act", lambda e, r=r, bank=bank: e.activation(out=pT[r][:], in_=self.pb[bank][:, 0:QB], func=AF.Exp, scale=1.0 / math.sqrt(128.0)),
                             reads=[("pb", bank)], writes=[("pT", r)])

                        def pv(e, kb=kb, r=r, kv=kv, ob_=ob_, db_=db_):
                            e.matmul(self.pb[ob_][:, 0:QB], lhsT=vv[:, kb * c.KVW + kv * 128:kb * c.KVW + (kv + 1) * 128], rhs=pT[r][:], start=(kb == 0), stop=(kb == FS - 1))
                            return e.matmul(self.pb[db_][:, 0:QB], lhsT=self.onesb[:], rhs=pT[r][:], start=(kb == 0), stop=(kb == FS - 1))
                        S.op("pe", pv, reads=["vv", ("pT", r), "onesb"], writes=[("pb", ob_), ("pb", db_)])
                    S.op("act", lambda e, n=n, ob_=ob_: e.activation(out=o32[n][:], in_=self.pb[ob_][:, 0:QB], func=AF.Copy), reads=[("pb", ob_)], writes=[("o32", n)])
                    S.op("act", lambda e, n=n, db_=db_: e.activation(out=d32[n][:], in_=self.pb[db_][:, 0:QB], func=AF.Copy), reads=[("pb", db_)], writes=[("d32", n)])
                    S.op("dve", lambda e, n=n: e.reciprocal(d32[n][:], d32[n][:]), reads=[("d32", n)], writes=[("d32", n)])
                    S.op("dve", lambda e, n=n: e.tensor_tensor(o32[n][:], o32[n][:], d32[n][:], ALU.mult), reads=[("o32", n), ("d32", n)], writes=[("o32", n)])
                    S.op("pool", lambda e, n=n: e.tensor_copy(ob[n][:], o32[n][:]), reads=[("o32", n)], writes=[("ob", n)])
                    S.op("sp", lambda e, n=n, h=h, qb=qb: e.dma_start(out=d["AOT"][:, h * NLOC + qb * QB:h * NLOC + (qb + 1) * QB], in_=ob[n][:]), reads=[("ob", n)], writes=[("AOT", h, qb)], dma=True)
            self.end_stage()

    def stage_filters(self):
        c, S, d = self.c, self.S, self.dram
        n, SS, HW, CB, NCB = c.SEQ, c.SS, c.HW, c.CB, c.NCB
        TB = min(512, n); NTB = n // TB
        with contextlib.ExitStack() as st:
            sb = lambda name, shape, dt=F32: self.sb(name, shape, dt, st)
            zt = sb("zt", [33, n]); w1 = sb("w1", [33, 64]); w2 = sb("w2", [64, 64]); w3 = sb("w3", [64, 64])
            fwo = sb("fwo", [64, 4 * HW]); prm = sb("prm", [64, 8])
            aa = [sb("aa%d" % i, [64, n]) for i in range(2)]
            s1 = sb("s1", [64, TB]); s2 = sb("s2", [64, TB])
            lag = sb("lag", [128, 3 * SS]); adl = sb("adl", [128, HW])
            hsb = sb("hsb", [128, SS * CB], BF16); hdb = sb("hdb", [128, SS * CB], BF16)
            dec = sb("dec", [128, CB]); ta = sb("ta", [128, CB]); tb_ = sb("tb", [128, CB]); tc_ = sb("tc", [128, CB]); td = sb("td", [128, CB])
            ksb = [sb("ksb%d" % i, [128, 3 * CB]) for i in range(2)]
            ld = lambda dst, src, key: S.op("sp", lambda e: e.dma_start(out=dst, in_=src), writes=[key], dma=True)
            ld(zt[:], d["zt"], "zt"); ld(w1[:], d["fw1"], "w1"); ld(w2[:], d["fw2"], "w2"); ld(w3[:], d["fw3"], "w3"); ld(fwo[:], d["fwo"], "fwo")
            ld(lag[:], d["lagc"], "lag"); ld(adl[:], d["adelta"].partition_broadcast(128), "adl")
            for i, nm in enumerate(["fb1", "fb2", "fb3", "ffreq"]):
                ld(prm[:, i:i + 1], d[nm].rearrange("(h o) -> h o", o=1), "prm")
            S.op("dve", lambda e: e.tensor_scalar(prm[:, 4:5], prm[:, 3:4], 1.0 / 3.0, None, ALU.mult), reads=["prm"], writes=["prm"])
            for l in range(3):
                S.op("dve", lambda e, l=l: e.tensor_tensor(prm[:, 5 + l:6 + l], prm[:, l:l + 1], prm[:, 4:5], ALU.mult), reads=["prm"], writes=["prm"])
            layers = [(w1, 33, zt, "zt"), (w2, 64, aa[0], ("aa", 0)), (w3, 64, aa[1], ("aa", 1))]
            outs = [(aa[0], ("aa", 0)), (aa[1], ("aa", 1)), (aa[0], ("aa", 0))]
            for l in range(3):
                w, K_, src, skey = layers[l]
                dst, dkey = outs[l]
                for tb in range(NTB):
                    bank = tb % 2
                    S.op("pe", lambda e, w=w, K_=K_, src=src, tb=tb, bank=bank: e.matmul(self.pb[bank][0:64, 0:TB], lhsT=w[0:K_, :], rhs=src[0:K_, tb * TB:(tb + 1) * TB], start=True, stop=True),
                         reads=["w1", "w2", "w3", skey], writes=[("pb", bank)])
                    S.op("act", lambda e, l=l, bank=bank: e.activation(out=s1[:], in_=self.pb[bank][0:64, 0:TB], func=AF.Sin, scale=prm[:, 4:5], bias=prm[:, 5 + l:6 + l]),
                         reads=[("pb", bank), "prm"], writes=["s1"])
                    S.op("dve", lambda e: e.tensor_tensor(s2[:], s1[:], s1[:], ALU.mult), reads=["s1"], writes=["s2"])
                    S.op("dve", lambda e: e.tensor_scalar(s2[:], s2[:], -4.0, 3.0, ALU.mult, ALU.add), reads=["s2"], writes=["s2"])
                    S.op("dve", lambda e, dst=dst, tb=tb: e.tensor_tensor(dst[:, tb * TB:(tb + 1) * TB], s2[:], s1[:], ALU.mult), reads=["s1", "s2"], writes=[dkey])
            a3, a3k = outs[2]
            ki = 0
            for o in range(2):
                for cb in range(NCB):
                    for lc in range(SS):
                        for dr in range(2):
                            col = o * 2 * HW + dr * HW + cb * CB
                            S.op("pe", lambda e, lc=lc, col=col, dr=dr: e.matmul(self.pb[dr][:, 0:CB], lhsT=a3[:, lc * 128:(lc + 1) * 128], rhs=fwo[:, col:col + CB], start=True, stop=True),
                                 reads=[a3k, "fwo"], writes=[("pb", dr)])
                        S.op("act", lambda e, lc=lc, cb=cb: e.activation(out=dec[:], in_=adl[:, cb * CB:(cb + 1) * CB], func=AF.Exp, scale=lag[:, lc:lc + 1]), reads=["adl", "lag"], writes=["dec"])
                        S.op("act", lambda e: e.activation(out=ta[:], in_=self.pb[0][:, 0:CB], func=AF.Copy), reads=[("pb", 0)], writes=["ta"])
                        S.op("act", lambda e: e.activation(out=tb_[:], in_=self.pb[1][:, 0:CB], func=AF.Copy), reads=[("pb", 1)], writes=["tb"])
                        S.op("dve", lambda e: e.tensor_tensor(ta[:], ta[:], dec[:], ALU.mult), reads=["ta", "dec"], writes=["ta"])
                        S.op("pool", lambda e: e.tensor_tensor(tb_[:], tb_[:], dec[:], ALU.mult), reads=["tb", "dec"], writes=["tb"])
                        S.op("pool", lambda e, lc=lc: e.tensor_tensor(hdb[:, lc * CB:(lc + 1) * CB], ta[:], tb_[:], ALU.subtract), reads=["ta", "tb"], writes=[("hdb", lc)])
                        S.op("dve", lambda e, lc=lc: e.tensor_scalar(tc_[:], tb_[:], lag[:, 2 * SS + lc:2 * SS + lc + 1], None, ALU.mult), reads=["tb", "lag"], writes=["tc"])
                        S.op("dve", lambda e, lc=lc: e.scalar_tensor_tensor(td[:], ta[:], lag[:, SS + lc:SS + lc + 1], tc_[:], ALU.mult, ALU.add), reads=["ta", "tc", "lag"], writes=["td"])
                        S.op("pool", lambda e, lc=lc: e.tensor_copy(hsb[:, lc * CB:(lc + 1) * CB], td[:]), reads=["td"], writes=[("hsb", lc)])
                    allh = [("hsb", lc) for lc in range(SS)] + [("hdb", lc) for lc in range(SS)]
                    for fi in range(SS):
                        tv, tk = self.wload(d["tabf"][fi], None, n=2 * SS * 128, q="sp")

                        b2, b3, b4 = (2, 3, 4) if fi % 2 == 0 else (5, 6, 7)

                        def f(e, tv=tv, fi=fi, b2=b2, b3=b3, b4=b4):
                            ins = None
                            for lc in range(SS):
                                ins = e.matmul(self.pb[b2][:, 0:CB], lhsT=tv[:, lc * 128:(lc + 1) * 128], rhs=hsb[:, lc * CB:(lc + 1) * CB], start=(lc == 0), stop=(lc == SS - 1))
                            for lc in range(SS):
                                ins = e.matmul(self.pb[b3][:, 0:CB], lhsT=tv[:, (SS + lc) * 128:(SS + lc + 1) * 128], rhs=hdb[:, lc * CB:(lc + 1) * CB], start=(lc == 0), stop=(lc == SS - 1))
                            if fi == 0:
                                for lc in range(SS):
                                    ins = e.matmul(self.pb[b4][:, 0:CB], lhsT=tv[:, (SS + lc) * 128:(SS + lc + 1) * 128], rhs=hsb[:, lc * CB:(lc + 1) * CB], start=(lc == 0), stop=(lc == SS - 1))
                            return ins
                        S.op("pe", f, reads=[tk] + allh, writes=[("pb", b2), ("pb", b3), ("pb", b4)])
                        kt_, kk = ksb[ki % 2], ("ksb", ki % 2); ki += 1
                        S.op("act", lambda e, kt_=kt_, b2=b2: e.activation(out=kt_[:, 0:CB], in_=self.pb[b2][:, 0:CB], func=AF.Copy), reads=[("pb", b2)], writes=[kk])
                        S.op("act", lambda e, kt_=kt_, b3=b3: e.activation(out=kt_[:, CB:2 * CB], in_=self.pb[b3][:, 0:CB], func=AF.Copy), reads=[("pb", b3)], writes=[kk])
                        S.op("act", lambda e, kt_=kt_, b2=b2: e.activation(out=kt_[:, 2 * CB:3 * CB], in_=self.pb[b2][:, 0:CB], func=AF.Copy), reads=[("pb", b2)], writes=[kk])
                        if fi == 0:
                            S.op("dve", lambda e, kt_=kt_: e.memset(kt_[0:1, CB:2 * CB], 0.0), writes=[kk])
                            S.op("act", lambda e, kt_=kt_, b4=b4: e.activation(out=kt_[0:1, 2 * CB:3 * CB], in_=self.pb[b4][0:1, 0:CB], func=AF.Copy), reads=[("pb", b4)], writes=[kk])
                        S.op("sp", lambda e, kt_=kt_, o=o, cb=cb, fi=fi: e.dma_start(out=d["KSP"][o, cb, fi], in_=kt_[:]), reads=[kk], writes=[("KSP", o, cb, fi)], dma=True)
            self.end_stage()

    def stage_hyena(self):
        c, S, d = self.c, self.S, self.dram
        n, SS, LS, HW, CB, NCB, NLOC = c.SEQ, c.SS, c.LS, c.HW, c.CB, c.NCB, c.NLOC
        NJ = CB // 128
        with contextlib.ExitStack() as st:
            sb = lambda name, shape, dt=F32: self.sb(name, shape, dt, st)
            wt = sb("wt", [128, 4 * CB]); hb = sb("hb", [128, 2 * CB])
            uv = sb("uv", [128, SS * CB], BF16); x1s = sb("x1s", [128, SS * CB], BF16); x2s = sb("x2s", [128, LS * CB], BF16)
            yre = sb("yre", [128, SS * CB], BF16); yim = sb("yim", [128, SS * CB], BF16)
            sh = [[sb("sh%d_%d" % (i, j), [128, CB]) for j in range(3)] for i in range(2)]
            t1 = sb("t1", [128, CB]); t2 = sb("t2", [128, CB]); t3 = sb("t3", [128, CB]); t4 = sb("t4", [128, CB])
            ure_l = [sb("ure%d" % i, [128, CB]) for i in range(2)]; uim_l = [sb("uim%d" % i, [128, CB]) for i in range(2)]
            kt2 = [sb("kk%d" % i, [128, 3 * CB]) for i in range(2)]
            y32_l = [sb("y32_%d" % i, [128, CB]) for i in range(2)]; hy = sb("hy", [128, CB]); hys = sb("hys", [128, CB], BF16)
            li = 0
            ki = 0
            for cb in range(NCB):
                for o in range(2):
                    S.op("sp", lambda e, o=o, cb=cb: e.dma_start(out=hb[:, o * CB:(o + 1) * CB], in_=d["hbias"][o, cb * CB:(cb + 1) * CB].partition_broadcast(128)), writes=["hb"], dma=True)
                for part in range(3):
                    for j in range(3):
                        S.op("sp", lambda e, part=part, j=j, cb=cb: e.dma_start(out=wt[:, j * CB:(j + 1) * CB],
                             in_=d["short_w"][j, part * HW + cb * CB:part * HW + (cb + 1) * CB].partition_broadcast(128)), writes=["wt"], dma=True)
                    S.op("sp", lambda e, part=part, cb=cb: e.dma_start(out=wt[:, 3 * CB:4 * CB],
                         in_=d["short_b"][part * HW + cb * CB:part * HW + (cb + 1) * CB].partition_broadcast(128)), writes=["wt"], dma=True)
                    for tcx in range(SS if part < 2 else LS):
                        i = li % 2; li += 1
                        for j in range(3):
                            S.op("sp", lambda e, i=i, j=j, tcx=tcx, part=part, cb=cb: e.dma_start(out=sh[i][j][:],
                                 in_=d["PHY"][tcx * 128 + j:tcx * 128 + j + 128, part * HW + cb * CB:part * HW + (cb + 1) * CB]),
                                 reads=[("PHYall",)], writes=[("sh", i, j)], dma=True)
                        W = lambda j: wt[:, j * CB:(j + 1) * CB]
                        a, b, cc = sh[i]
                        ka, kb_, kc = ("sh", i, 0), ("sh", i, 1), ("sh", i, 2)
                        S.op("pool", lambda e, a=a, W=W: e.tensor_tensor(a[:], a[:], W(0), ALU.mult), reads=[ka, "wt"], writes=[ka])
                        S.op("dve", lambda e, b=b, W=W: e.tensor_tensor(b[:], b[:], W(1), ALU.mult), reads=[kb_, "wt"], writes=[kb_])
                        S.op("dve", lambda e, cc=cc, W=W: e.tensor_tensor(cc[:], cc[:], W(2), ALU.mult), reads=[kc, "wt"], writes=[kc])
                        S.op("pool", lambda e, a=a, b=b: e.tensor_tensor(a[:], a[:], b[:], ALU.add), reads=[ka, kb_], writes=[ka])
                        S.op("pool", lambda e, cc=cc, W=W: e.tensor_tensor(cc[:], cc[:], W(3), ALU.add), reads=[kc, "wt"], writes=[kc])
                        if part == 0:
                            S.op("dve", lambda e, a=a, cc=cc: e.tensor_tensor(a[:], a[:], cc[:], ALU.add), reads=[ka, kc], writes=[ka])
                            S.op("act", lambda e, a=a, tcx=tcx: e.activation(out=uv[:, tcx * CB:(tcx + 1) * CB], in_=a[:], func=AF.Copy), reads=[ka], writes=[("uv", tcx)])
                        else:
                            dstt = x1s if part == 1 else x2s
                            S.op("dve", lambda e, a=a, cc=cc, dstt=dstt, tcx=tcx: e.tensor_tensor(dstt[:, tcx * CB:(tcx + 1) * CB], a[:], cc[:], ALU.add),
                                 reads=[ka, kc], writes=[("xs", part, tcx)])
                for o in range(2):
                    alluv = [("uv", t) for t in range(SS)]
                    for fi in range(SS):
                        tv, tk = self.wload(d["tabf"][fi], None, n=2 * SS * 128, q="sp")
                        kt_, kk = kt2[ki % 2], ("kk", ki % 2); ki += 1
                        S.op("sp", lambda e, kt_=kt_, o=o, cb=cb, fi=fi: e.dma_start(out=kt_[:], in_=d["KSP"][o, cb, fi]), reads=[("KSPall",)], writes=[kk], dma=True)

                        pa, pbk = (0, 1) if fi % 2 == 0 else (4, 5)
                        ure, uim = ure_l[fi % 2], uim_l[fi % 2]
                        kur, kui = ("ure", fi % 2), ("uim", fi % 2)

                        def f(e, tv=tv, pa=pa, pbk=pbk):
                            ins = None
                            for t in range(SS):
                                ins = e.matmul(self.pb[pa][:, 0:CB], lhsT=tv[:, t * 128:(t + 1) * 128], rhs=uv[:, t * CB:(t + 1) * CB], start=(t == 0), stop=(t == SS - 1))
                            for t in range(SS):
                                ins = e.matmul(self.pb[pbk][:, 0:CB], lhsT=tv[:, (SS + t) * 128:(SS + t + 1) * 128], rhs=uv[:, t * CB:(t + 1) * CB], start=(t == 0), stop=(t == SS - 1))
                            return ins
                        S.op("pe", f, reads=[tk] + alluv, writes=[("pb", pa), ("pb", pbk)])
                        S.op("act", lambda e, ure=ure, pa=pa: e.activation(out=ure[:], in_=self.pb[pa][:, 0:CB], func=AF.Copy), reads=[("pb", pa)], writes=[kur])
                        S.op("act", lambda e, uim=uim, pbk=pbk: e.activation(out=uim[:], in_=self.pb[pbk][:, 0:CB], func=AF.Copy), reads=[("pb", pbk)], writes=[kui])
                        S.op("dve", lambda e, kt_=kt_, ure=ure: e.tensor_tensor(t1[:], ure[:], kt_[:, 0:CB], ALU.mult), reads=[kur, kk], writes=["t1"])
                        S.op("pool", lambda e, kt_=kt_, uim=uim: e.tensor_tensor(t2[:], uim[:], kt_[:, CB:2 * CB], ALU.mult), reads=[kui, kk], writes=["t2"])
                        S.op("dve", lambda e, kt_=kt_, ure=ure: e.tensor_tensor(t3[:], ure[:], kt_[:, CB:2 * CB], ALU.mult), reads=[kur, kk], writes=["t3"])
                        S.op("pool", lambda e, kt_=kt_, uim=uim: e.tensor_tensor(t4[:], uim[:], kt_[:, 2 * CB:3 * CB], ALU.mult), reads=[kui, kk], writes=["t4"])
                        S.op("dve", lambda e, fi=fi: e.tensor_tensor(yre[:, fi * CB:(fi + 1) * CB], t1[:], t2[:], ALU.subtract), reads=["t1", "t2"], writes=[("yre", fi)])
                        S.op("pool", lambda e, fi=fi: e.tensor_tensor(yim[:, fi * CB:(fi + 1) * CB], t3[:], t4[:], ALU.add), reads=["t3", "t4"], writes=[("yim", fi)])
                    ally = [("yre", f_) for f_ in range(SS)] + [("yim", f_) for f_ in range(SS)]
                    for ti in range(SS if o == 0 else LS):
                        tv, tk = self.wload(d["tabg"][ti], None, n=2 * SS * 128, q="sp")

                        pg = 2 if ti % 2 == 0 else 6
                        y32, ky = y32_l[ti % 2], ("y32", ti % 2)

                        def g(e, tv=tv, pg=pg):
                            ins = None
                            for f_ in range(SS):
                                ins = e.matmul(self.pb[pg][:, 0:CB], lhsT=tv[:, f_ * 128:(f_ + 1) * 128], rhs=yre[:, f_ * CB:(f_ + 1) * CB], start=(f_ == 0), stop=False)
                            for f_ in range(SS):
                                ins = e.matmul(self.pb[pg][:, 0:CB], lhsT=tv[:, (SS + f_) * 128:(SS + f_ + 1) * 128], rhs=yim[:, f_ * CB:(f_ + 1) * CB], start=False, stop=(f_ == SS - 1))
                            return ins
                        S.op("pe", g, reads=[tk] + ally, writes=[("pb", pg)])
                        S.op("act", lambda e, y32=y32, pg=pg: e.activation(out=y32[:], in_=self.pb[pg][:, 0:CB], func=AF.Copy), reads=[("pb", pg)], writes=[ky])
                        S.op("dve", lambda e, ti=ti, o=o: e.tensor_tensor(t1[:], uv[:, ti * CB:(ti + 1) * CB], hb[:, o * CB:(o + 1) * CB], ALU.mult), reads=[("uv", ti), "hb"], writes=["t1"])
                        S.op("pool", lambda e, y32=y32: e.tensor_tensor(y32[:], y32[:], t1[:], ALU.add), reads=[ky, "t1"], writes=[ky])
                        if o == 0:
                            S.op("dve", lambda e, ti=ti, y32=y32: e.tensor_tensor(uv[:, ti * CB:(ti + 1) * CB], y32[:], x1s[:, ti * CB:(ti + 1) * CB], ALU.mult),
                                 reads=[ky, ("xs", 1, ti)], writes=[("uv", ti)])
                        else:
                            S.op("dve", lambda e, ti=ti, y32=y32: e.tensor_tensor(hy[:], y32[:], x2s[:, ti * CB:(ti + 1) * CB], ALU.mult), reads=[ky, ("xs", 2, ti)], writes=["hy"])

                            def tr(e):
                                ins = None
                                for j in range(NJ):
                                    ins = e.transpose(self.pb[3][:, j * 128:(j + 1) * 128], hy[:, j * 128:(j + 1) * 128], self.ident[:])
                                return ins
                            S.op("pe", tr, reads=["hy", "ident"], writes=[("pb", 3)])
                            S.op("act", lambda e: e.activation(out=hys[:], in_=self.pb[3][:, 0:CB], func=AF.Copy), reads=[("pb", 3)], writes=["hys"])
                            dst = d["HYT"].rearrange("p (h t) -> p h t", h=c.HC)[:, cb * NJ:(cb + 1) * NJ, ti * 128:(ti + 1) * 128]
                            S.op("sp", lambda e, dst=dst: e.dma_start(out=dst, in_=hys[:, :].rearrange("p (h t) -> p h t", h=NJ)), reads=["hys"], writes=[("HYT", cb, ti)], dma=True)
            self.end_stage()

    def stage_merge(self):
        c, S, d = self.c, self.S, self.dram
        D, KT, LS, NLOC = c.D, c.KT, c.LS, c.NLOC
        NA, NH_ = c.NQ, c.HC
        NJB = D // 512
        with contextlib.ExitStack() as st:
            sb = lambda name, shape, dt=F32: self.sb(name, shape, dt, st)
            TS = min(4, LS); T = TS * 128
            ao = sb("ao", [128, NA * T], BF16); hyv = sb("hyv", [128, NH_ * T], BF16)
            mT = sb("mT", [128, KT * T], BF16)
            ga = [sb("ga%d" % i, [128, 512]) for i in range(2)]; gh = [sb("gh%d" % i, [128, 512]) for i in range(2)]
            a32 = [sb("a32_%d" % i, [128, 512]) for i in range(4)]; h32 = [sb("h32_%d" % i, [128, 512]) for i in range(2)]
            fb = [sb("fb%d" % i, [128, 512]) for i in range(3)]; junk = sb("junk", [128, 512], BF16)
            S.op("dve", lambda e: e.memset(self.ssq[:], 0.0), writes=["ssq"])
            gi = 0; fbi = 0; ai = 0
            for t0 in range(0, LS, TS):
                S.op("sp", lambda e, t0=t0: e.dma_start(out=ao[:, :].rearrange("p (h t) -> p h t", h=NA), in_=d["AOT"].rearrange("p (h t) -> p h t", h=NA)[:, :, t0 * 128:t0 * 128 + T]),
                     reads=[("AOTall",)], writes=["ao"], dma=True)
                S.op("sp", lambda e, t0=t0: e.dma_start(out=hyv[:, :].rearrange("p (h t) -> p h t", h=NH_), in_=d["HYT"].rearrange("p (h t) -> p h t", h=NH_)[:, :, t0 * 128:t0 * 128 + T]),
                     reads=[("HYTall",)], writes=["hyv"], dma=True)
                for jb in range(NJB):
                    aslot = {}
                    for br, (w, src_t, nk, skey) in enumerate(((d["w_bra"], ao, NA, "ao"), (d["w_brh"], hyv, NH_, "hyv"))):
                        kg = min(8, nk); nq = nk // kg
                        bufs = []
                        for q in range(nq):
                            src = w[q * kg * 128:(q + 1) * kg * 128, jb * 512:(jb + 1) * 512].rearrange("(kt p) j -> p kt j", p=128)
                            bufs.append(self.wload(src, "p (kt j) -> p kt j", n=kg * 512, kt=kg))
                        for s_ in range(TS):
                            bank = s_

                            def f(e, bufs=bufs, s_=s_, bank=bank, src_t=src_t, kg=kg, nq=nq):
                                ins = None
                                for q in range(nq):
                                    for kt in range(kg):
                                        k = q * kg + kt
                                        ins = e.matmul(self.pb[bank][:, :], lhsT=src_t[:, k * T + s_ * 128:k * T + (s_ + 1) * 128], rhs=bufs[q][0][:, kt, :],
                                                       start=(k == 0), stop=(k == nq * kg - 1))
                                return ins
                            S.op("pe", f, reads=[b[1] for b in bufs] + [skey], writes=[("pb", bank)])
                            r0 = (t0 + s_) * 128
                            if br == 0:
                                i = ai % 4; ai += 1
                                aslot[s_] = i
                                g_ = gi % 2
                                S.op("sp", lambda e, g_=g_, r0=r0, jb=jb: e.dma_start(out=ga[g_][:], in_=d["GATES"][r0:r0 + 128, jb * 512:(jb + 1) * 512]), reads=[("GATESall",)], writes=[("ga", g_)], dma=True)
                                S.op("act", lambda e, i=i, bank=bank: e.activation(out=a32[i][:], in_=self.pb[bank][:, :], func=AF.Copy), reads=[("pb", bank)], writes=[("a32", i)])
                                S.op("dve", lambda e, i=i, g_=g_: e.tensor_tensor(a32[i][:], a32[i][:], ga[g_][:], ALU.mult), reads=[("a32", i), ("ga", g_)], writes=[("a32", i)])
                            else:
                                i = aslot[s_]
                                g_ = gi % 2; gi += 1
                                S.op("sp", lambda e, g_=g_, r0=r0, jb=jb: e.dma_start(out=gh[g_][:], in_=d["GATES"][r0:r0 + 128, D + jb * 512:D + (jb + 1) * 512]), reads=[("GATESall",)], writes=[("gh", g_)], dma=True)
                                S.op("act", lambda e, g_=g_, bank=bank: e.activation(out=h32[g_][:], in_=self.pb[bank][:, :], func=AF.Copy), reads=[("pb", bank)], writes=[("h32", g_)])
                                S.op("dve", lambda e, g_=g_: e.tensor_tensor(h32[g_][:], h32[g_][:], gh[g_][:], ALU.mult), reads=[("h32", g_), ("gh", g_)], writes=[("h32", g_)])
                                S.op("dve", lambda e, i=i, g_=g_: e.tensor_tensor(a32[i][:], a32[i][:], h32[g_][:], ALU.add), reads=[("a32", i), ("h32", g_)], writes=[("a32", i)])
                                tb = 4 + s_

                                def ft(e, i=i, tb=tb):
                                    ins = None
                                    for k in range(4):
                                        ins = e.transpose(self.pb[tb][:, k * 128:(k + 1) * 128], a32[i][:, k * 128:(k + 1) * 128], self.ident[:])
                                    return ins
                                S.op("pe", ft, reads=[("a32", i), "ident"], writes=[("pb", tb)])
                                for k in range(4):
                                    kk_ = jb * 4 + k
                                    S.op("act", lambda e, s_=s_, k=k, kk_=kk_, tb=tb: e.activation(out=mT[:, kk_ * T + s_ * 128:kk_ * T + (s_ + 1) * 128],
                                         in_=self.pb[tb][:, k * 128:(k + 1) * 128], func=AF.Copy), reads=[("pb", tb)], writes=[("mT", s_, kk_)])
                allm = [("mT", s_, k) for s_ in range(TS) for k in range(KT)]
                kg = min(8, KT); nq = KT // kg
                for jb in range(NJB):
                    bufs = []
                    for q in range(nq):
                        src = d["w_out"][q * kg * 128:(q + 1) * kg * 128, jb * 512:(jb + 1) * 512].rearrange("(kt p) j -> p kt j", p=128)
                        bufs.append(self.wload(src, "p (kt j) -> p kt j", n=kg * 512, kt=kg))
                    for s_ in range(TS):
                        bank = (jb * TS + s_) % 4

                        def f(e, bufs=bufs, s_=s_, bank=bank):
                            ins = None
                            for q in range(nq):
                                for kt in range(kg):
                                    k = q * kg + kt
                                    ins = e.matmul(self.pb[bank][:, :], lhsT=mT[:, k * T + s_ * 128:k * T + (s_ + 1) * 128], rhs=bufs[q][0][:, kt, :], start=(k == 0), stop=(k == KT - 1))
                            return ins
                        S.op("pe", f, reads=[b[1] for b in bufs] + allm, writes=[("pb", bank)])
                        si = t0 + s_
                        S.op("act", lambda e, bank=bank, si=si, jb=jb: e.activation(out=junk[:], in_=self.pb[bank][:, :], func=AF.Square, accum_out=self.ssq[:, si, jb:jb + 1]),
                             reads=[("pb", bank), "ssq"], writes=["junk", "ssq"])
                        fbt, fk = fb[fbi % 3], ("fb", fbi % 3); fbi += 1
                        S.op("act", lambda e, bank=bank, fbt=fbt: e.activation(out=fbt[:], in_=self.pb[bank][:, :], func=AF.Copy), reads=[("pb", bank)], writes=[fk])
                        S.op("sp", lambda e, fbt=fbt, si=si, jb=jb: e.dma_start(out=d["F"][si * 128:(si + 1) * 128, jb * 512:(jb + 1) * 512], in_=fbt[:]), reads=[fk], writes=[("F", si)], dma=True)
            self.end_stage()

    def end_stage(self):
        self.S.barrier()
        self.S.flush()

    def wbuf(self):
        i = self.wi
        self.wi = (self.wi + 1) % NB
        return self.wp[i], [("wp", i, 0), ("wp", i, 1)]

    def wload(self, src, shape_str=None, n=None, q="pool", **kw):
        t, key = self.wbuf()
        view = t[:, 0:n]
        if shape_str:
            view = view.rearrange(shape_str, **kw)
        self.S.op(q, lambda e, view=view, src=src: e.dma_start(out=view, in_=src), writes=[key], dma=True)
        return view, key

    def wload2(self, src_rows, c0, W, KG):
        t, key = self.wbuf()
        view = t[:, 0:KG * W].rearrange("p (kt j) -> p kt j", kt=KG)
        if KG < 2:
            src = src_rows[:, c0:c0 + W].rearrange("(kt p) j -> p kt j", p=128)
            self.S.op("pool", lambda e: e.dma_start(out=view, in_=src), writes=[key], dma=True)
            return view, key
        h = KG // 2
        keys = []
        for i, (a, b) in enumerate(((0, h), (h, KG))):
            src = src_rows[a * 128:b * 128, c0:c0 + W].rearrange("(kt p) j -> p kt j", p=128)
            k = key[i]
            self.S.op("pool", lambda e, a=a, b=b, src=src: e.dma_start(out=view[:, a:b, :], in_=src), writes=[k], dma=True)
            keys.append(k)
        return view, keys

    def load_fm(self, vec2d, nrows, dst, dkey, tmp, pbank):
        S = self.S
        S.op("sp", lambda e: e.dma_start(out=tmp[0:nrows, :], in_=vec2d), writes=["lf_tmp"], dma=True)
        S.op("pe", lambda e: e.transpose(self.pb[pbank][:, 0:nrows], tmp[0:nrows, :], self.ident[0:nrows, 0:nrows]),
             reads=["lf_tmp", "ident"], writes=[("pb", pbank)])
        S.op("dve", lambda e: e.tensor_copy(dst, self.pb[pbank][:, 0:nrows]), reads=[("pb", pbank)], writes=[dkey])

    def stage_adaln(self):
        c, S, d, nc = self.c, self.S, self.dram, self.nc
        D, KT = c.D, c.KT
        with contextlib.ExitStack() as st:
            sb = lambda name, shape, dt=F32: self.sb(name, shape, dt, st)
            tmp = sb("lf_tmp", [128, 128]); cfm = sb("cfm", [128, 2 * KT]); sT = sb("sT", [128, KT, 2], BF16)
            bfm = sb("bfm", [128, 9 * KT]); prefm = sb("prefm", [128, 3 * KT]); postfm = sb("postfm", [128, 3 * KT])
            assert 2 * KT <= 128
            self.load_fm(d["cvec"].rearrange("c (k i) -> (c k) i", i=128), 2 * KT, cfm[:], "cfm", tmp, 0)
            S.op("act", lambda e: e.activation(out=sT[:].rearrange("p k c -> p c k"),
                                               in_=cfm[:].rearrange("p (c k) -> p c k", c=2), func=AF.Silu),
                 reads=["cfm"], writes=["sT"])
            nrow = 9 * KT
            r0 = 0
            bi = 0
            while r0 < nrow:
                n = min(96, nrow - r0)
                self.load_fm(d["b_ada"].rearrange("(r i) -> r i", i=128)[r0:r0 + n, :], n, bfm[:, r0:r0 + n], "bfm", tmp, 1 + bi % 2)
                r0 += n; bi += 1
            self.load_fm(d["pre_g"].rearrange("s (k i) -> (s k) i", i=128), 3 * KT, prefm[:], "prefm", tmp, 1)
            self.load_fm(d["post_g"].rearrange("s (k i) -> (s k) i", i=128), 3 * KT, postfm[:], "postfm", tmp, 2)
            nblk = (9 * D) // 512
            KG = min(8, KT)
            NQ = KT // KG
            for blk in range(nblk):
                bufs = []
                for q in range(NQ):
                    src = d["w_ada"][q * KG * 128:(q + 1) * KG * 128, blk * 512:(blk + 1) * 512].rearrange("(kt p) j -> p kt j", p=128)
                    bufs.append(self.wload(src, "p (kt j) -> p kt j", n=KG * 512, kt=KG))
                for j in range(4):
                    cc = blk * 4 + j
                    bank, col = (cc * 2) // 512, (cc * 2) % 512

                    def f(e, bufs=bufs, j=j, bank=bank, col=col):
                        ins = None
                        for q in range(NQ):
                            for kt in range(KG):
                                ins = e.matmul(self.pb[bank][:, col:col + 2], lhsT=bufs[q][0][:, kt, j * 128:(j + 1) * 128],
                                               rhs=sT[:, q * KG + kt, :], start=(q == 0 and kt == 0), stop=(q == NQ - 1 and kt == KG - 1))
                        return ins
                    S.op("pe", f, reads=[b[1] for b in bufs] + ["sT"], writes=[("pb", bank)])
            ncc = 9 * KT
            for bank in range((ncc * 2 + 511) // 512):
                c0 = bank * 256
                n = min(256, ncc - c0)
                for cond in range(2):
                    S.op("dve", lambda e, bank=bank, c0=c0, n=n, cond=cond: e.tensor_tensor(
                        self.Mfm[:, c0:c0 + n, cond], self.pb[bank][:, 0:2 * n].rearrange("p (c t) -> p c t", t=2)[:, :, cond],
                        bfm[:, c0:c0 + n], ALU.add), reads=[("pb", bank), "bfm"], writes=["Mfm"])
            for s in range(3):
                for cond in range(2):
                    S.op("dve", lambda e, s=s, cond=cond: e.scalar_tensor_tensor(
                        self.Gm[:, s, :, cond], self.Mfm[:, (3 * s + 1) * KT:(3 * s + 2) * KT, cond], 1.0,
                        prefm[:, s * KT:(s + 1) * KT], ALU.add, ALU.mult), reads=["Mfm", "prefm"], writes=["Gm"])
                    S.op("dve", lambda e, s=s, cond=cond: e.scalar_tensor_tensor(
                        self.PGm[:, s, :, cond], self.Mfm[:, (3 * s + 2) * KT:(3 * s + 3) * KT, cond], (1.0 if s == 1 else 0.5),
                        postfm[:, s * KT:(s + 1) * KT], ALU.mult, ALU.mult), reads=["Mfm", "postfm"], writes=["PGm"])
            self.end_stage()

    def stage_pre(self, src, subs, s):
        c, S, d = self.c, self.S, self.dram
        D, KT = c.D, c.KT
        with contextlib.ExitStack() as st:
            sb = lambda name, shape, dt=F32: self.sb(name, shape, dt, st)
            xt = [sb("xt%d" % i, [128, D]) for i in range(2)]
            junk = sb("junk", [128, D], BF16)
            hts = [sb("hts%d" % i, [128, KT, 128], BF16) for i in range(2)]
            st1 = sb("st1", [128, 2 * len(subs)])
            for n, (si, cond) in enumerate(subs):
                x, xk = xt[n % 2], ("xt", n % 2)
                ht, hk = hts[n % 2], ("hts", n % 2)
                S.op("sp", lambda e, x=x, si=si: e.dma_start(out=x[:], in_=src[si * 128:(si + 1) * 128, :]), writes=[xk], dma=True)
                S.op("dve", lambda e, n=n: e.memset(st1[:, 2 * n:2 * n + 2], 0.0), writes=[("st1", n)])
                S.op("act", lambda e, x=x, n=n: e.activation(out=junk[:], in_=x[:], func=AF.Square, accum_out=st1[:, 2 * n:2 * n + 1]),
                     reads=[xk, ("st1", n)], writes=["junk", ("st1", n)])
                S.op("dve", lambda e, n=n: e.tensor_scalar(st1[:, 2 * n + 1:2 * n + 2], st1[:, 2 * n:2 * n + 1], 1.0 / D, 1e-6, ALU.mult, ALU.add),
                     reads=[("st1", n)], writes=[("st1", n)])
                S.op("act", lambda e, n=n: e.activation(out=st1[:, 2 * n + 1:2 * n + 2], in_=st1[:, 2 * n + 1:2 * n + 2], func=AF.Sqrt),
                     reads=[("st1", n)], writes=[("st1", n)])
                S.op("dve", lambda e, n=n: e.reciprocal(st1[:, 2 * n + 1:2 * n + 2], st1[:, 2 * n + 1:2 * n + 2]),
                     reads=[("st1", n)], writes=[("st1", n)])
                S.op("act", lambda e, x=x, n=n: e.activation(out=x[:], in_=x[:], func=AF.Copy, scale=st1[:, 2 * n + 1:2 * n + 2]),
                     reads=[xk, ("st1", n)], writes=[xk])
                for k4 in range(0, KT, 4):
                    bank = (k4 // 4) % 8
                    nk = min(4, KT - k4)

                    def f(e, x=x, k4=k4, bank=bank, nk=nk):
                        ins = None
                        for k in range(nk):
                            ins = e.transpose(self.pb[bank][:, k * 128:(k + 1) * 128], x[:, (k4 + k) * 128:(k4 + k + 1) * 128], self.ident[:])
                        return ins
                    S.op("pe", f, reads=[xk, "ident"], writes=[("pb", bank)])
                    for k in range(nk):
                        kt = k4 + k
                        if kt % 2 == 0:
                            S.op("act", lambda e, ht=ht, kt=kt, k=k, bank=bank, cond=cond: e.activation(
                                out=ht[:, kt, :], in_=self.pb[bank][:, k * 128:(k + 1) * 128], func=AF.Identity,
                                scale=self.Gm[:, s, kt, cond:cond + 1], bias=self.Mfm[:, 3 * s * KT + kt, cond:cond + 1]),
                                reads=[("pb", bank), "Gm", "Mfm"], writes=[hk])
                        else:
                            S.op("dve", lambda e, ht=ht, kt=kt, k=k, bank=bank, cond=cond: e.tensor_scalar(
                                ht[:, kt, :], self.pb[bank][:, k * 128:(k + 1) * 128],
                                self.Gm[:, s, kt, cond:cond + 1], self.Mfm[:, 3 * s * KT + kt, cond:cond + 1], ALU.mult, ALU.add),
                                reads=[("pb", bank), "Gm", "Mfm"], writes=[hk])
                S.op("sp", lambda e, ht=ht, si=si: e.dma_start(out=d["HT"][:, :, si * 128:(si + 1) * 128], in_=ht[:]),
                     reads=[hk], writes=[("HT", si)], dma=True)
            self.end_stage()

    def stage_ffn(self, which, nsub):
        c, S, d = self.c, self.S, self.dram
        D, KT, FC = c.D, c.KT, c.FC
        wg, wu, wd = d["wg"][which], d["wu"][which], d["wd"][which]
        KG = min(16, KT); NH = KT // KG
        DQ = min(1024, D); NDQ = D // DQ; HBW = min(512, DQ); NHB = DQ // HBW
        assert FC % 2 == 0
        import os
        with contextlib.ExitStack() as st:
            sb = lambda name, shape, dt=F32: self.sb(name, shape, dt, st)
            hT = sb("hT", [128, KT, 512], BF16); gT2 = sb("gT", [128, FC * 512], BF16)
            gT = gT2[:, :].rearrange("p (c t) -> p c t", c=FC)
            gX = sb("gX", [128, 512], BF16)
            sg = [sb("sg%d" % i, [128, 512]) for i in range(2)]
            fb = [sb("fb%d" % i, [128, 512]) for i in range(3)]
            junk = sb("junk", [128, 512], BF16)
            S.op("dve", lambda e: e.memset(self.ssq[:], 0.0), writes=["ssq"])
            fbi = 0
            for t0 in range(0, nsub, 4):
                ts = min(4, nsub - t0); T = ts * 128
                assert ts * NHB <= 8
                S.op("sp", lambda e, t0=t0, T=T: e.dma_start(out=hT[:, :, 0:T], in_=d["HT"][:, :, t0 * 128:t0 * 128 + T]),
                     reads=[("HT", t0 + i) for i in range(ts)], writes=["hT"], dma=True)
                for grp in range(FC // 2 if not os.environ.get('MK_NOA') else 0):
                    set_ = grp % 2
                    bufs = {}
                    for h in range(NH):
                        for m, w in (("g", wg), ("u", wu)):
                            bufs[(m, h)] = self.wload2(w[h * KG * 128:(h + 1) * KG * 128, :], grp * 256, 256, KG)
                    banks = {("g", 0): set_ * 4, ("g", 1): set_ * 4 + 1, ("u", 0): set_ * 4 + 2, ("u", 1): set_ * 4 + 3}

                    def f(e, bufs=bufs, banks=banks, T=T):
                        ins = None
                        for h in range(NH):
                            for kt in range(KG):
                                for m in ("g", "u"):
                                    for j in range(2):
                                        ins = e.matmul(self.pb[banks[(m, j)]][:, 0:T], lhsT=bufs[(m, h)][0][:, kt, j * 128:(j + 1) * 128],
                                                       rhs=hT[:, h * KG + kt, 0:T], start=(h == 0 and kt == 0), stop=(h == NH - 1 and kt == KG - 1))
                        return ins
                    S.op("pe", f, reads=[k_ for b in bufs.values() for k_ in b[1]] + ["hT"], writes=[("pb", b) for b in banks.values()])
                    for j in range(2 if not os.environ.get('MK_NOEV') else 0):
                        ch = grp * 2 + j
                        S.op("act", lambda e, j=j, banks=banks, T=T: e.activation(out=sg[j][:, 0:T], in_=self.pb[banks[("g", j)]][:, 0:T], func=AF.Sigmoid),
                             reads=[("pb", banks[("g", j)])], writes=[("sg", j)])
                        if os.environ.get('MK_EV') == '1':
                            continue
                        S.op("dve", lambda e, j=j, banks=banks, T=T: e.tensor_tensor(sg[j][:, 0:T], self.pb[banks[("g", j)]][:, 0:T], sg[j][:, 0:T], ALU.mult),
                             reads=[("sg", j), ("pb", banks[("g", j)])], writes=[("sg", j)])
                        if os.environ.get('MK_EV') == '2':
                            continue
                        S.op("dve", lambda e, j=j, banks=banks, T=T: e.tensor_tensor(sg[j][:, 0:T], self.pb[banks[("u", j)]][:, 0:T], sg[j][:, 0:T], ALU.mult),
                             reads=[("sg", j), ("pb", banks[("u", j)])], writes=[("sg", j)])
                        S.op("act", lambda e, j=j, T=T, ch=ch: e.activation(out=gT2[:, ch * 512:ch * 512 + T], in_=sg[j][:, 0:T], func=AF.Copy),
                             reads=[("sg", j)], writes=[("gT", ch)])
                if os.environ.get('MK_BAR'):
                    S.barrier()
                import os
                for dq in range(NDQ if not os.environ.get('MK_NOB') else 0):
                    for c2 in range(FC // 2):
                        src = wd[c2 * 256:(c2 + 1) * 256, dq * DQ:(dq + 1) * DQ].rearrange("(cc p) j -> p cc j", p=128)
                        wv, wk = self.wload(src, "p (cc j) -> p cc j", n=2 * DQ, cc=2)

                        def f(e, wv=wv, c2=c2, ts=ts):
                            ins = None
                            for cc in range(2):
                                ch = c2 * 2 + cc
                                for s_ in range(ts):
                                    for hb in range(NHB):
                                        ins = e.matmul(self.pb[s_ * NHB + hb][:, 0:HBW], lhsT=gT2[:, ch * 512 + s_ * 128:ch * 512 + (s_ + 1) * 128],
                                                       rhs=wv[:, cc, hb * HBW:(hb + 1) * HBW], start=(ch == 0), stop=(ch == FC - 1))
                            return ins
                        S.op("pe", f, reads=[wk] + [("gT", c2 * 2), ("gT", c2 * 2 + 1)], writes=[("pb", s_ * NHB + hb) for s_ in range(ts) for hb in range(NHB)])
                    BL = int(os.environ.get('MK_B', '9'))
                    for s_ in range(ts):
                        for hb in range(NHB):
                            bank = s_ * NHB + hb
                            si = t0 + s_
                            col = dq * NHB + hb
                            if BL >= 2:
                                S.op("act", lambda e, bank=bank, si=si, col=col: e.activation(out=junk[:, 0:HBW], in_=self.pb[bank][:, 0:HBW], func=AF.Square,
                                                                                           accum_out=self.ssq[:, si, col:col + 1]),
                                     reads=[("pb", bank), "ssq"], writes=["junk", "ssq", ("pbr", bank)])
                            fbt, fk = fb[fbi % 3], ("fb", fbi % 3); fbi += 1
                            if BL >= 3:
                                S.op("act", lambda e, bank=bank, fbt=fbt: e.activation(out=fbt[:, 0:HBW], in_=self.pb[bank][:, 0:HBW], func=AF.Copy), reads=[("pb", bank)], writes=[fk])
                            if BL >= 4:
                                S.op("sp", lambda e, fbt=fbt, si=si, col=col: e.dma_start(out=d["F"][si * 128:(si + 1) * 128, col * HBW:(col + 1) * HBW], in_=fbt[:, 0:HBW]),
                                     reads=[fk], writes=[("F", si)], dma=True)
            self.end_stage()

    def bcast_from_fm(self, col_ap_fn, dst, dkey, diag, nkt):
        S = self.S
        for k4 in range(0, nkt, 4):
            bank = (k4 // 4) % 8
            nk = min(4, nkt - k4)
            for k in range(nk):
                dg, dk = diag[k % 2], ("diag", k % 2)
                S.op("dve", lambda e, dg=dg, kt=k4 + k: e.tensor_scalar(dg[:], self.ident[:], col_ap_fn(kt), None, ALU.mult),
                     reads=["ident", "PGm", "Mfm"], writes=[dk])
                S.op("pe", lambda e, dg=dg, k=k, bank=bank: e.matmul(self.pb[bank][:, k * 128:(k + 1) * 128], lhsT=self.ones32[:], rhs=dg[:], start=True, stop=True),
                     reads=[dk, "ones32"], writes=[("pb", bank)])
            S.op("act", lambda e, bank=bank, k4=k4, nk=nk: e.activation(out=dst[:, k4 * 128:(k4 + nk) * 128], in_=self.pb[bank][:, 0:nk * 128], func=AF.Copy),
                 reads=[("pb", bank)], writes=[dkey])

    def stage_post(self, xsrc, dst, subs, s, final=False):
        c, S, d = self.c, self.S, self.dram
        D, KT = c.D, c.KT
        NCOL = D // min(512, D)
        with contextlib.ExitStack() as st:
            sb = lambda name, shape, dt=F32: self.sb(name, shape, dt, st)
            self.ones32 = sb("ones32", [128, 128])
            diag = [sb("diag%d" % i, [128, 128]) for i in range(2)]
            S.op("dve", lambda e: e.memset(self.ones32[:], 1.0), writes=["ones32"])
            conds = sorted(set(cd for _, cd in subs))
            pgb = {}
            for cd in conds:
                pgb[cd] = sb("pgb%d" % cd, [128, D])
                self.bcast_from_fm(lambda kt, cd=cd: self.PGm[:, s, kt, cd:cd + 1], pgb[cd], ("pgb", cd), diag, KT)
            xt = [sb("xt%d" % i, [128, D]) for i in range(2)]
            ft = [sb("ft%d" % i, [128, D]) for i in range(2)]
            S.op("dve", lambda e: e.tensor_reduce(self.rstd[:, 0:len(subs)], self.ssq[:, 0:len(subs), 0:NCOL], AX.X, ALU.add), reads=["ssq"], writes=["rstd"])
            S.op("dve", lambda e: e.tensor_scalar(self.rstd[:, 0:len(subs)], self.rstd[:, 0:len(subs)], 1.0 / D, 1e-6, ALU.mult, ALU.add), reads=["rstd"], writes=["rstd"])
            S.op("act", lambda e: e.activation(out=self.rstd[:, 0:len(subs)], in_=self.rstd[:, 0:len(subs)], func=AF.Sqrt), reads=["rstd"], writes=["rstd"])
            S.op("dve", lambda e: e.reciprocal(self.rstd[:, 0:len(subs)], self.rstd[:, 0:len(subs)]), reads=["rstd"], writes=["rstd"])
            outs = []
            for n, (si, cd) in enumerate(subs):
                x, xk = xt[n % 2], ("xt", n % 2)
                f, fk = ft[n % 2], ("ft", n % 2)
                S.op("sp", lambda e, x=x, si=si: e.dma_start(out=x[:], in_=xsrc[si * 128:(si + 1) * 128, :]), writes=[xk], dma=True)
                S.op("sp", lambda e, f=f, si=si: e.dma_start(out=f[:], in_=d["F"][si * 128:(si + 1) * 128, :]), reads=[("F", si)], writes=[fk], dma=True)
                eng = "pool"
                S.op("dve", lambda e, f=f, n=n, cd=cd: e.scalar_tensor_tensor(f[:], f[:], self.rstd[:, n:n + 1], pgb[cd][:], ALU.mult, ALU.mult),
                     reads=[fk, "rstd", ("pgb", cd)], writes=[fk])
                S.op(eng, lambda e, f=f, x=x: e.tensor_tensor(x[:], x[:], f[:], ALU.add), reads=[fk, xk], writes=[xk])
                outs.append(S.op("sp", lambda e, x=x, si=si: e.dma_start(out=dst[si * 128:(si + 1) * 128, :], in_=x[:]), reads=[xk], writes=[("dst", si)], dma=True))
            self.end_stage()


def host_consts(c, half):
    n = c.SEQ
    bf = ml_dtypes.bfloat16
    out = {}
    out["ident"] = np.eye(128, dtype=np.float32)
    out["onesb"] = np.ones((128, 128), dtype=bf)
    pos = np.arange(n)
    if half == 1:
        pos = pos[::-1]
    row = (pos // c.GRID_W).astype(np.float64)
    col = (pos % c.GRID_W).astype(np.float64)
    inv = 10000.0 ** (-np.arange(0, 64, 2, dtype=np.float64) / 64)
    ar, ac = row[:, None] * inv[None], col[:, None] * inv[None]
    cos = np.concatenate([np.cos(ar), np.cos(ar), np.cos(ac), np.cos(ac)], 1)
    sin = np.concatenate([-np.sin(ar), np.sin(ar), -np.sin(ac), np.sin(ac)], 1)
    cosf = np.concatenate([np.ones((c.CTX, 128)), cos], 0)
    sinf = np.concatenate([np.zeros((c.CTX, 128)), sin], 0)
    out["ropec"] = np.tile(cosf, (1, 4)).astype(np.float32)
    out["ropes"] = np.tile(sinf, (1, 4)).astype(np.float32)
    t = np.arange(n, dtype=np.float64)
    f = np.arange(n, dtype=np.float64)
    ang = np.pi * np.outer(t, f) / n
    C = np.cos(ang)
    Sm = -np.sin(ang)
    Sm[:, 0] = (-1.0) ** t
    SS = c.SS
    tf = np.stack([C, Sm], 0).reshape(2, SS, 128, SS, 128)
    out["tabf"] = np.ascontiguousarray(tf.transpose(3, 2, 0, 1, 4)).reshape(SS, 128, 2 * SS * 128).astype(bf)
    Gc = C.T / n
    Gs = -np.sin(ang).T / n
    Gc[0, :] = 1.0 / (2 * n)
    Gs[0, :] = ((-1.0) ** t) / (2 * n)
    tg = np.stack([Gc, Gs], 0).reshape(2, SS, 128, SS, 128)
    out["tabg"] = np.ascontiguousarray(tg.transpose(3, 2, 0, 1, 4)).reshape(SS, 128, 2 * SS * 128).astype(bf)
    tl = np.linspace(0.0, 1.0, n, dtype=np.float32)
    bands = 16
    w = (2.0 * np.pi * np.arange(n, dtype=np.float32) / n).astype(np.float32)
    fr = np.linspace(1e-4, bands - 1, bands, dtype=np.float32)
    z = np.concatenate([tl[:, None], np.cos(fr[None] * w[:, None]), -np.sin(fr[None] * w[:, None])], 1)
    out["zt"] = np.ascontiguousarray(z.T).astype(np.float32)
    lag = np.zeros((128, 3 * SS), np.float32)
    lag[:, 0:SS] = -tl.reshape(SS, 128).T
    mA = np.ones(n, np.float32); mB = np.ones(n, np.float32)
    if half == 0:
        mB[0] = 0.0
    else:
        mA[0] = 0.0
    lag[:, SS:2 * SS] = mA.reshape(SS, 128).T
    lag[:, 2 * SS:3 * SS] = mB.reshape(SS, 128).T
    out["lagc"] = lag
    max_decay = math.log(1e-2) / 0.3
    min_decay = math.log(1e-2) / 1.5
    out["adelta"] = np.abs(np.linspace(min_decay, max_decay, c.HW, dtype=np.float32)).astype(np.float32)
    return out


def prep_core(inp, b, half, c, consts):
    n, HW = c.SEQ, c.HW
    f32 = lambda a: np.ascontiguousarray(np.asarray(a, dtype=np.float32))
    xl = np.asarray(inp["x"][b])
    if half == 1:
        xl = xl[::-1]
    m = dict(consts)
    m["xfull"] = f32(np.concatenate([np.asarray(inp["ctx"][b]), xl], 0))
    m["cvec"] = f32(np.stack([np.asarray(inp["c"][b]), np.asarray(inp["c_ctx"])], 0))
    m["w_ada"] = f32(inp["w_ada"][0]); m["b_ada"] = f32(inp["b_ada"][0])
    m["pre_g"] = f32(inp["pre_g"][0]); m["post_g"] = f32(inp["post_g"][0])
    m["wg"] = f32(inp["ffn_w_gate"][0]); m["wu"] = f32(inp["ffn_w_up"][0]); m["wd"] = f32(inp["ffn_w_down"][0])
    m["w_in"] = f32(inp["w_in"][0])
    m["qg4"] = f32(np.tile(np.asarray(inp["q_norm_g"][0]), 4)); m["kg4"] = f32(np.tile(np.asarray(inp["k_norm_g"][0]), 4))
    sw = np.asarray(inp["short_w"][0])
    m["short_w"] = f32(sw[::-1] if half == 1 else sw)
    m["short_b"] = f32(inp["short_b"][0])
    m["fw1"] = f32(inp["filt_w1"][0]); m["fb1"] = f32(inp["filt_b1"][0]); m["fw2"] = f32(inp["filt_w2"][0]); m["fb2"] = f32(inp["filt_b2"][0])
    m["fw3"] = f32(inp["filt_w3"][0]); m["fb3"] = f32(inp["filt_b3"][0]); m["ffreq"] = f32(inp["filt_freq"][0])
    wo = np.asarray(inp["filt_w_out"][0]).reshape(64, 2, 2, HW)
    if half == 1:
        wo = wo[:, :, ::-1, :]
    m["fwo"] = f32(wo.reshape(64, 4 * HW))
    m["hbias"] = f32(inp["hyena_bias"][0])
    m["w_bra"] = f32(inp["w_br_attn"][0]); m["w_brh"] = f32(inp["w_br_hyena"][0]); m["w_out"] = f32(inp["w_out"][0])
    return m


_CACHE = {}


def kernel(**inputs):
    cfg = Cfg()
    if "nc" not in _CACHE:
        mk = MK(cfg)
        _CACHE["nc"] = mk.build()
        _CACHE["used"] = set(mk.dram.keys())
        _CACHE["consts"] = [host_consts(cfg, 0), host_consts(cfg, 1)]
    nc, used, consts = _CACHE["nc"], _CACHE["used"], _CACHE["consts"]
    in_maps = []
    for core in range(8):
        b, half = core // 2, core % 2
        m = prep_core(inputs, b, half, cfg, consts[half])
        in_maps.append({k: v for k, v in m.items() if k in used})
    res = run_bass_kernel_spmd(nc, in_maps, core_ids=list(range(8)))
    out = np.zeros((cfg.B, cfg.SEQ, cfg.D), np.float32)
    for core in range(8):
        b, half = core // 2, core % 2
        o = np.asarray(res.results[core]["out"], dtype=np.float32)
        if half == 0:
            out[b, :cfg.NLOC] = o
        else:
            out[b, cfg.NLOC:] = o[::-1]
    return out
```
